# Optimizing a Trainium2 kernel written in Bass

```python
import math
import jax, jax.numpy as jnp
from jax import lax
import numpy as np

D_MODEL = 2048
BATCH = 2
SEQ = 4096
DEPTH = 1

D_MIX = D_MODEL
DA_WIDTH = D_MIX // 2
DA_HEADS = 8
DA_V_DIM = DA_WIDTH // DA_HEADS
DA_HEAD_DIM = DA_V_DIM // 2
HG_WIDTH = D_MIX - DA_WIDTH
HG_HEADS = 8
HG_V_DIM = HG_WIDTH // HG_HEADS
HG_K_DIM = 128
HG_KEY_WIDTH = HG_HEADS * HG_K_DIM
HG_CHUNK = 64
SPLIT_SIZES = (DA_WIDTH, DA_WIDTH, DA_WIDTH, HG_KEY_WIDTH, HG_KEY_WIDTH, HG_WIDTH, HG_WIDTH)
D_IN = sum(SPLIT_SIZES)
Q_BLOCK = 128
NUM_BUCKETS = 32
MAX_DISTANCE = 128
N_MEM = 256
MEM_HEADS = 4
MEM_HEAD_DIM = D_MODEL // MEM_HEADS
D_FF = 5632
CONV_WIDTH = 3
EPS = 1e-6

kernel_name = "hymba_diffattn_hgrn2_convffn_block"


def rmsnorm(x, w):
    xf = x.astype(jnp.float32)
    y = xf * lax.rsqrt(jnp.mean(xf * xf, axis=-1, keepdims=True) + EPS)
    return (y * w.astype(jnp.float32)).astype(x.dtype)


def t5_bucket(rel):
    n = jnp.maximum(rel, 0)
    max_exact = NUM_BUCKETS // 2
    nf = jnp.maximum(n, 1).astype(jnp.float32)
    large = max_exact + (jnp.log(nf / max_exact) / math.log(MAX_DISTANCE / max_exact)
                         * (NUM_BUCKETS - max_exact)).astype(jnp.int32)
    large = jnp.minimum(large, NUM_BUCKETS - 1)
    return jnp.where(n < max_exact, n, large)


def diff_attention(q, k, v, lam, bias_table):
    B, S = q.shape[0], q.shape[1]
    nb = S // Q_BLOCK
    scale = DA_HEAD_DIM ** -0.5
    qb = q.reshape(B, nb, Q_BLOCK, DA_HEADS, 2, DA_HEAD_DIM).transpose(1, 0, 3, 4, 2, 5)
    kt = k.transpose(0, 2, 3, 1, 4)
    vt = v.transpose(0, 2, 1, 3)
    kpos = jnp.arange(S)

    def block(args):
        i, qi = args
        qpos = i * Q_BLOCK + jnp.arange(Q_BLOCK)
        rel = qpos[:, None] - kpos[None, :]
        bias = bias_table[t5_bucket(rel)].transpose(2, 0, 1).astype(jnp.float32)
        logits = jnp.einsum('bhmqd,bhmkd->bhmqk', qi, kt).astype(jnp.float32) * scale
        logits = logits + bias[None, :, None]
        logits = jnp.where(rel >= 0, logits, -1e30)
        p = jax.nn.softmax(logits, axis=-1)
        a = p[:, :, 0] - lam * p[:, :, 1]
        return jnp.einsum('bhqk,bhkv->bhqv', a.astype(v.dtype), vt)

    out = lax.map(block, (jnp.arange(nb), qb))
    return out.transpose(1, 0, 3, 2, 4).reshape(B, S, DA_HEADS, DA_V_DIM)


def hgrn2(q, fz, i, lb):
    B, S = q.shape[0], q.shape[1]
    nc = S // HG_CHUNK
    f32 = jnp.float32
    f = lb + (1.0 - lb) * jax.nn.sigmoid(fz.astype(f32))
    log_f = jnp.log(f)
    kk = 1.0 - f

    def to_chunks(t):
        return t.astype(f32).reshape(B, nc, HG_CHUNK, HG_HEADS, t.shape[-1]).transpose(1, 0, 3, 2, 4)

    qc, kc, vc, gc = to_chunks(q), to_chunks(kk), to_chunks(i), to_chunks(log_f)
    causal = jnp.tril(jnp.ones((HG_CHUNK, HG_CHUNK), dtype=bool))[:, :, None]

    def step(state, inp):
        q_c, k_c, v_c, g_c = inp
        G = jnp.cumsum(g_c, axis=2)
        o_inter = jnp.einsum('bhck,bhkv->bhcv', q_c * jnp.exp(G), state)
        diff = G[:, :, :, None, :] - G[:, :, None, :, :]
        decay = jnp.where(causal, jnp.exp(jnp.where(causal, diff, 0.0)), 0.0)
        a = jnp.einsum('bhtk,bhsk,bhtsk->bhts', q_c, k_c, decay)
        o_intra = jnp.einsum('bhts,bhsv->bhtv', a, v_c)
        g_last = G[:, :, -1]
        k_dec = k_c * jnp.exp(g_last[:, :, None, :] - G)
        new_state = jnp.exp(g_last)[..., None] * state + jnp.einsum('bhsk,bhsv->bhkv', k_dec, v_c)
        return new_state, o_inter + o_intra

    s0 = jnp.zeros((B, HG_HEADS, HG_K_DIM, HG_V_DIM), f32)
    _, o = lax.scan(step, s0, (qc, kc, vc, gc))
    return o.transpose(1, 0, 3, 2, 4).reshape(B, S, HG_HEADS, HG_V_DIM).astype(i.dtype)


def causal_dwconv(u, w, b):
    S = u.shape[1]
    up = jnp.pad(u, ((0, 0), (CONV_WIDTH - 1, 0), (0, 0)))
    return sum(w[j] * up[:, j:j + S] for j in range(CONV_WIDTH)) + b


def split_points():
    pts, acc = [], 0
    for s in SPLIT_SIZES[:-1]:
        acc += s
        pts.append(acc)
    return pts


def setup_inputs(seed: int = 0) -> dict:
    key = jax.random.key(seed)
    ks = jax.random.split(key, 24)
    f32 = jnp.float32

    def nrm(k, shape, scale):
        return jax.random.normal(k, shape, f32) * scale

    def gain(k, shape):
        return 1.0 + 0.1 * jax.random.normal(k, shape, f32)

    L = DEPTH
    return {
        "x": nrm(ks[0], (BATCH, SEQ, D_MODEL), 1.0),
        "mem": nrm(ks[1], (BATCH, N_MEM, D_MODEL), 1.0),
        "w_in": nrm(ks[2], (L, D_MODEL, D_IN), D_MODEL ** -0.5),
        "w_out": nrm(ks[3], (L, D_MIX, D_MODEL), D_MIX ** -0.5),
        "norm_mix_w": gain(ks[4], (L, D_MODEL)),
        "lam_q1": nrm(ks[5], (L, DA_HEAD_DIM), 0.1),
        "lam_k1": nrm(ks[6], (L, DA_HEAD_DIM), 0.1),
        "lam_q2": nrm(ks[7], (L, DA_HEAD_DIM), 0.1),
        "lam_k2": nrm(ks[8], (L, DA_HEAD_DIM), 0.1),
        "da_subln_w": gain(ks[9], (L, DA_V_DIM)),
        "hg_lb_raw": nrm(ks[10], (L + 1, HG_KEY_WIDTH), 0.5),
        "hg_norm_w": gain(ks[11], (L, HG_V_DIM)),
        "rel_bias": nrm(ks[12], (NUM_BUCKETS, DA_HEADS), 0.5),
        "norm_mem_w": gain(ks[13], (L, D_MODEL)),
        "mem_kv_norm_w": gain(ks[14], (L, D_MODEL)),
        "w_mq": nrm(ks[15], (L, D_MODEL, D_MODEL), D_MODEL ** -0.5),
        "w_mkv": nrm(ks[16], (L, D_MODEL, 2 * D_MODEL), D_MODEL ** -0.5),
        "w_mo": nrm(ks[17], (L, D_MODEL, D_MODEL), D_MODEL ** -0.5),
        "norm_ffn_w": gain(ks[18], (L, D_MODEL)),
        "w_up": nrm(ks[19], (L, D_MODEL, 2 * D_FF), D_MODEL ** -0.5),
        "conv_w": nrm(ks[20], (L, CONV_WIDTH, 2 * D_FF), CONV_WIDTH ** -0.5),
        "conv_b": nrm(ks[21], (L, 2 * D_FF), 0.02),
        "w_down": nrm(ks[22], (L, D_FF, D_MODEL), D_FF ** -0.5),
        "final_norm_w": gain(ks[23], (D_MODEL,)),
    }


def reference(x, mem, w_in, w_out, norm_mix_w, lam_q1, lam_k1, lam_q2, lam_k2, da_subln_w,
              hg_lb_raw, hg_norm_w, rel_bias, norm_mem_w, mem_kv_norm_w, w_mq, w_mkv, w_mo,
              norm_ffn_w, w_up, conv_w, conv_b, w_down, final_norm_w):
    B, S, _ = x.shape
    f32 = jnp.float32
    lb_all = jnp.cumsum(jax.nn.softmax(hg_lb_raw.astype(f32), axis=0), axis=0)
    pts = split_points()
    for layer in range(DEPTH):
        h = rmsnorm(x, norm_mix_w[layer])
        proj = h @ w_in[layer]
        da_q, da_k, da_v, hg_q, hg_f, hg_i, hg_g = jnp.split(proj, pts, axis=-1)

        lam_init = 0.8 - 0.6 * math.exp(-0.3 * layer)
        lam = (jnp.exp(jnp.sum(lam_q1[layer].astype(f32) * lam_k1[layer].astype(f32)))
               - jnp.exp(jnp.sum(lam_q2[layer].astype(f32) * lam_k2[layer].astype(f32)))
               + lam_init)
        da_o = diff_attention(da_q.reshape(B, S, DA_HEADS, 2, DA_HEAD_DIM),
                              da_k.reshape(B, S, DA_HEADS, 2, DA_HEAD_DIM),
                              da_v.reshape(B, S, DA_HEADS, DA_V_DIM), lam, rel_bias)
        da_o = rmsnorm(da_o, da_subln_w[layer]) * (1.0 - lam_init)

        lb = lb_all[layer].reshape(HG_HEADS, HG_K_DIM)
        hg_o = hgrn2(hg_q.reshape(B, S, HG_HEADS, HG_K_DIM),
                     hg_f.reshape(B, S, HG_HEADS, HG_K_DIM),
                     hg_i.reshape(B, S, HG_HEADS, HG_V_DIM), lb)
        hg_o = rmsnorm(hg_o, hg_norm_w[layer]) * jax.nn.silu(hg_g.reshape(B, S, HG_HEADS, HG_V_DIM))

        mix = jnp.concatenate([da_o.reshape(B, S, DA_WIDTH), hg_o.reshape(B, S, HG_WIDTH)], axis=-1)
        x = x + mix @ w_out[layer]

        hq = rmsnorm(x, norm_mem_w[layer])
        mk = rmsnorm(mem, mem_kv_norm_w[layer])
        mq = (hq @ w_mq[layer]).reshape(B, S, MEM_HEADS, MEM_HEAD_DIM)
        m_k, m_v = jnp.split(mk @ w_mkv[layer], 2, axis=-1)
        m_k = m_k.reshape(B, N_MEM, MEM_HEADS, MEM_HEAD_DIM)
        m_v = m_v.reshape(B, N_MEM, MEM_HEADS, MEM_HEAD_DIM)
        m_logits = jnp.einsum('bqhd,bkhd->bhqk', mq, m_k).astype(f32) * (MEM_HEAD_DIM ** -0.5)
        m_p = jax.nn.softmax(m_logits, axis=-1).astype(x.dtype)
        m_o = jnp.einsum('bhqk,bkhd->bqhd', m_p, m_v).reshape(B, S, D_MODEL)
        x = x + m_o @ w_mo[layer]

        hf = rmsnorm(x, norm_ffn_w[layer])
        u = causal_dwconv(hf @ w_up[layer], conv_w[layer], conv_b[layer])
        a, b = jnp.split(u, 2, axis=-1)
        x = x + (jax.nn.silu(a) * b) @ w_down[layer]
    return rmsnorm(x, final_norm_w)
```

```python
import math
from contextlib import ExitStack
import numpy as np
import concourse.bass as bass
import concourse.mybir as mybir
from concourse.bass_utils import run_bass_kernel_spmd

F32 = mybir.dt.float32
BF16 = mybir.dt.bfloat16
AF = mybir.ActivationFunctionType
ALU = mybir.AluOpType
AX = mybir.AxisListType

D = 2048
S = 4096
KC = 16
NMEM = 256
DFF = 5632
EPS = 1e-6
NT2 = 1026
TILES2 = [(2, 512), (514, 512), (0, 2)]
TOKP = S + 2
GROUPS = [12, 12, 12, 8]
NEG = -1.0e30

C_NMW = 0
C_LAM = 16
C_SUBW = 272
C_HGW = 273
C_LBCM = 274
C_BFAR = 278
C_LBTM = 280
C_BIAS = 792
C_MASK = 2840
C_TG = 2840
C_TD = 2968
C1 = 3096
C2_NMEM = 0
C2_NKV = 16
C2_NFFN = 32
C2_NFIN = 48
C2_FLAG = 64
C2_CB = 65
C2_CW = 153
C2 = 153 + 264


class Sch:
    ENG = ("pe", "act", "dve", "pool", "sp")

    def __init__(self, nc, stack):
        self.nc = nc
        self.stack = stack
        self.lists = {e: [] for e in self.ENG}
        self.cnt = {e: 0 for e in self.ENG}
        self.seen = {e: {} for e in self.ENG}
        self.res = {}
        self.sem = {}
        self.dcnt = {}
        self.n = 0
        self.limit = LIMIT
        self.marks = []
        self.sem["cc"] = stack.enter_context(nc.semaphore("d_cc"))
        self.dcnt["cc"] = 0
        for e in ("pe", "act", "dve", "pool"):
            self.sem[e] = stack.enter_context(nc.semaphore("sem_" + e))

    def dsem(self, key):
        if key not in self.sem:
            self.sem[key] = self.stack.enter_context(self.nc.semaphore("d_" + key))
            self.dcnt[key] = 0
        return self.sem[key]

    def _deps(self, eng, reads, writes):
        need = {}

        def add(tok):
            if tok is None:
                return
            k, v = tok
            if need.get(k, 0) < v:
                need[k] = v

        for r in reads:
            e = self.res.get(r)
            if e:
                add(e["w"])
        for w in writes:
            e = self.res.get(w)
            if e:
                add(e["w"])
                for k, v in e["r"].items():
                    add((k, v))
        waits = []
        for k, v in need.items():
            if k == "pe" and eng == "pe":
                continue
            if self.seen[eng].get(k, 0) >= v:
                continue
            self.seen[eng][k] = v
            waits.append((k, v))
        return waits

    def _commit(self, tok, reads, writes):
        for w in writes:
            self.res[w] = {"w": tok, "r": {}}
        k, v = tok
        for r in reads:
            e = self.res.setdefault(r, {"w": None, "r": {}})
            if e["r"].get(k, 0) < v:
                e["r"][k] = v

    def mark(self, name):
        self.marks.append((name, self.n))

    def op(self, eng, fn, reads=(), writes=(), force=False):
        self.n += 1
        if self.limit and self.n > self.limit and not force:
            return
        waits = self._deps(eng, reads, writes)
        self.cnt[eng] += 1
        self._commit((eng, self.cnt[eng]), reads, writes)
        self.lists[eng].append((waits, fn, None))

    def dma(self, q, key, fn, reads=(), writes=(), inc=16, force=False):
        self.n += 1
        if self.limit and self.n > self.limit and not force:
            return
        waits = self._deps(q, reads, writes)
        self.dsem(key)
        self.dcnt[key] += inc
        self._commit((key, self.dcnt[key]), reads, writes)
        self.lists[q].append((waits, fn, (key, inc)))

    def barrier(self, skip=()):
        for e in self.ENG:
            waits = []
            for k in list(self.sem.keys()):
                if k in skip:
                    continue
                v = self.cnt[k] if k in self.cnt else self.dcnt[k]
                if v <= 0 or (k == e):
                    continue
                if self.seen[e].get(k, 0) >= v:
                    continue
                self.seen[e][k] = v
                waits.append((k, v))
            if waits:
                self.lists[e].append((waits, None, None))

    def final_wait(self, eng, keys):
        waits = [(k, self.dcnt[k]) for k in keys]
        self.lists[eng].append((waits, None, None))

    def flush(self, block):
        L = self.lists
        self.lists = {e: [] for e in self.ENG}
        sem = self.sem

        def run(name, e):
            for waits, fn, kind in L[name]:
                for k, v in waits:
                    e.wait_ge(sem[k], v)
                if fn is None:
                    continue
                ins = fn(e)
                if kind is None:
                    ins.then_inc(sem[name], 1)
                else:
                    ins.then_inc(sem[kind[0]], kind[1])

        if L["pe"]:
            @block.tensor
            def _(e):
                run("pe", e)
        if L["act"]:
            @block.scalar
            def _(e):
                run("act", e)
        if L["dve"]:
            @block.vector
            def _(e):
                run("dve", e)
        if L["pool"]:
            @block.gpsimd
            def _(e):
                run("pool", e)
        if L["sp"]:
            @block.sync
            def _(e):
                run("sp", e)


def MM(out, lhsT, rhs, start=True, stop=True):
    return lambda e: e.matmul(out, lhsT, rhs, start=start, stop=stop)


def ACT(out, in_, func, bias=None, scale=None):
    kw = {}
    if bias is not None:
        kw["bias"] = bias
    if scale is not None:
        kw["scale"] = scale
    return lambda e: e.activation(out, in_, func, **kw)


def TT(out, a, b, op):
    return lambda e: e.tensor_tensor(out, a, b, op)


def TS(out, a, s1, s2, op0, op1=None):
    if op1 is None:
        return lambda e: e.tensor_scalar(out, a, s1, None, op0)
    return lambda e: e.tensor_scalar(out, a, s1, s2, op0, op1)


def STT(out, in0, scalar, in1, op0, op1):
    return lambda e: e.scalar_tensor_tensor(out, in0, scalar, in1, op0, op1)


def CP(out, in_):
    return lambda e: e.tensor_copy(out, in_)


def DMA(out, in_):
    return lambda e: e.dma_start(out=out, in_=in_)


NJ = 8
NSLOT = 4
LIMIT = 0
MARKS = []


def build(dbg=None):
    nc = bass.Bass("TRN2", target_bir_lowering=False)

    def din(name, shape, dt=F32):
        return nc.dram_tensor(name, list(shape), dt, kind="ExternalInput").ap()

    xT1 = din("xT1", [8, 128, KC, 512])
    w1d = din("w1", [128, KC, 1792])
    cst1 = din("cst1", [128, C1])
    maskd = din("maskw", [128, 1024])
    if dbg is None:
        x2T = din("x2T", [128, KC, NT2])
        memT = din("memT", [128, KC, NMEM])
        cst2 = din("cst2", [128, C2])
        wo_d = din("w_out", [8, 128, KC, 256])
        wq_d = din("w_mq", [8, 128, KC, 256])
        wkv_d = din("w_mkv", [16, 128, KC, 256])
        wmo_d = din("w_mo", [8, 128, KC, 256])
        wup_d = din("w_up", [44, 128, KC, 256])
        wdn_d = din("w_down", [4, 8, 128, 12, 256])
    if dbg is None:
        yT = nc.dram_tensor("yT", [128, KC, 1024], F32, kind="ExternalOutput").ap()
    ib = nc.dram_tensor("ib", [512, TOKP], BF16, kind="Internal").ap()
    ob = nc.dram_tensor("ob", [8 * 512, TOKP], BF16, kind="Internal").ap()
    if dbg in ("mix", "sim"):
        dmix = nc.dram_tensor("dmix", [512, TOKP], BF16, kind="ExternalOutput").ap()

    with ExitStack() as gstack:
        sch = Sch(nc, gstack)
        ps = [gstack.enter_context(nc.psum_tensor("ps%d" % i, [128, 512], F32)) for i in range(8)]
        rot = [0]

        def rbank(lo=4, hi=8):
            b = lo + rot[0] % (hi - lo)
            rot[0] += 1
            return b

        with ExitStack() as st:
            def sb(name, shape, dt=F32):
                return st.enter_context(nc.sbuf_tensor(name, list(shape), dt))

            c1 = sb("c1", [128, C1])
            w1s = sb("w1s", [128, KC, 1792], BF16)
            xt = sb("xt", [128, KC, 256])
            hT = sb("hT", [128, KC, 512], BF16)
            sq = sb("sq", [128, 2, 512], BF16)
            rstd = sb("rstd", [128, 512])
            KT = sb("KT", [128, 2, S], BF16)
            Vt = sb("Vt", [128, 32, 256], BF16)
            QT = sb("QT", [128, 2, 512], BF16)
            hqT = sb("hqT", [128, 2, 512], BF16)
            snT = sb("snT", [128, 2, 512], BF16)
            sgT = sb("sgT", [128, 2, 512], BF16)
            vhg = sb("vhg", [128, 4, 256], BF16)
            ftm = sb("ftm", [128, 4, 256])
            kk = sb("kk", [128, 4, 256])
            eD = sb("eD", [128, 2, 128])
            kdec = sb("kdec", [128, 4, 256], BF16)
            eG = sb("eG", [128, 2, 512])
            qh = sb("qh", [128, 2, 512], BF16)
            kth = sb("kth", [128, 2, 512], BF16)
            Am = sb("Am", [128, 2, 512], BF16)
            Sf = sb("Sf", [128, 2, 128])
            Sbf = sb("Sbf", [128, 2, 128], BF16)
            Eb = sb("Eb", [128, 4, 512], BF16)
            Ssb = sb("Ssb", [128, 2, 512])
            r12 = sb("r12", [128, 2, 512])
            mixo = sb("mixo", [128, 4, 512], BF16)
            ones = sb("ones", [128, 128], BF16)
            onesD = sb("onesD", [128, 128], BF16)
            onesH = sb("onesH", [128, 128], BF16)
            sm = sb("sm", [128, 16])
            lbt = sb("lbt", [128, 2, 256])
            lbc = sb("lbc", [128, 4])
            zt = sb("zt", [128, 8], BF16)
            lamt = sb("lamt", [128, 64])

            sch.dma("sp", "c1", DMA(c1[:], cst1), writes=["c1"])
            for i in range(4):
                sch.dma("pool", "w1_%d" % i, DMA(w1s[:, 4 * i:4 * i + 4, :], w1d[:, 4 * i:4 * i + 4, :]),
                        writes=[("w1", i)])
            sch.op("dve", lambda e: e.memset(ones[:], 1.0), writes=["ones"])
            sch.op("dve", lambda e: e.memset(onesD[:], 1.0 / D), writes=["onesD"])
            sch.op("dve", lambda e: e.memset(onesH[:], 1.0 / 128), writes=["onesH"])
            sch.op("dve", lambda e: e.memset(Sf[:], 0.0), writes=[("Sf", 0), ("Sf", 1)])
            sch.op("dve", lambda e: e.memset(Sbf[:], 0.0), writes=[("Sbf", 0), ("Sbf", 1)])
            sch.op("dve", lambda e: e.memset(zt[:], 0.0), writes=["zt"])
            for c in range(4):
                sch.dma("sp", "ibz", DMA(ib[c * 128:(c + 1) * 128, 0:2], zt[:, 0:2]), reads=["zt"],
                        writes=[("ibz", c)])
            for i in range(2):
                sch.op("dve", TT(lamt[:], c1[:, C_LAM + 128 * i:C_LAM + 128 * i + 64],
                                 c1[:, C_LAM + 128 * i + 64:C_LAM + 128 * i + 128], ALU.mult),
                       reads=["c1"], writes=["lamt"])
                sch.op("dve", lambda e, i=i: e.reduce_sum(sm[:, i:i + 1], lamt[:], AX.X),
                       reads=["lamt"], writes=["sm"])
            sch.op("act", ACT(sm[:, 2:4], sm[:, 0:2], AF.Exp), reads=["sm"], writes=["sm"])
            sch.op("dve", STT(sm[:, 4:5], sm[:, 2:3], 0.2, sm[:, 3:4], ALU.add, ALU.subtract),
                   reads=["sm"], writes=["sm"])
            sch.op("dve", TS(sm[:, 5:6], sm[:, 4:5], -1.0, None, ALU.mult), reads=["sm"], writes=["sm"])
            sch.op("dve", TS(sm[:, 6:7], c1[:, C_SUBW:C_SUBW + 1], 0.8, None, ALU.mult),
                   reads=["c1", "sm"], writes=["sm"])
            sch.op("dve", lambda e: e.memset(sm[:, 7:8], EPS), reads=["sm"], writes=["epsb"])
            EPSB = sm[:, 7:8]
            NLAM = sm[:, 5:6]
            SUBW8 = sm[:, 6:7]
            sch.op("dve", TT(lbc[:, 0:2], c1[:, C_LBCM:C_LBCM + 2], c1[:, C_LBCM + 2:C_LBCM + 4], ALU.subtract),
                   reads=["c1"], writes=["lbc"])
            sch.op("act", ACT(lbc[:, 0:2], lbc[:, 0:2], AF.Sigmoid), reads=["lbc"], writes=["lbc"])
            sch.op("dve", TS(lbc[:, 2:4], lbc[:, 0:2], -1.0, 1.0, ALU.mult, ALU.add), reads=["lbc"], writes=["lbc"])
            sch.op("dve", TT(lbt[:, 0, :], c1[:, C_LBTM:C_LBTM + 256], c1[:, C_LBTM + 256:C_LBTM + 512],
                             ALU.subtract), reads=["c1"], writes=["lbt"])
            sch.op("act", ACT(lbt[:, 0, :], lbt[:, 0, :], AF.Sigmoid), reads=["lbt"], writes=["lbt"])
            sch.op("dve", TS(lbt[:, 1, :], lbt[:, 0, :], -1.0, 1.0, ALU.mult, ALU.add), reads=["lbt"],
                   writes=["lbt"])
            Sflat = Ssb[:].rearrange("p a n -> p (a n)")
            sch.dma("sp", "mask", DMA(Sflat, maskd), writes=[("Ssb", 0), ("Ssb", 1)])
            for h in range(2):
                sch.op("dve", TT(c1[:, C_BIAS + 1024 * h:C_BIAS + 1024 * (h + 1)],
                                 c1[:, C_BIAS + 1024 * h:C_BIAS + 1024 * (h + 1)],
                                 Sflat, ALU.add), reads=["c1", ("Ssb", 0), ("Ssb", 1)], writes=["c1"])
            TG = c1[:, C_TG:C_TG + 128]
            TD = c1[:, C_TD:C_TD + 128]
            SCALE = 0.125

            for j in range(NJ):
                sch.mark("j%d_norm" % j)
                for hf in range(2):
                    hsl = slice(hf * 256, (hf + 1) * 256)
                    sch.dma("sp", "xt", DMA(xt[:], xT1[j][:, :, hsl]), writes=["xt"])
                    bst = rbank()
                    for kc in range(KC):
                        sch.op("act", ACT(sq[:, kc % 2, 0:256], xt[:, kc, :], AF.Square), reads=["xt"],
                               writes=[("sq", kc % 2)])
                        sch.op("pe", MM(ps[bst][:, 0:256], onesD[:], sq[:, kc % 2, 0:256], kc == 0, kc == KC - 1),
                               reads=[("sq", kc % 2), "onesD"], writes=[("ps", bst)])
                    sch.op("act", ACT(rstd[:, hsl], ps[bst][:, 0:256], AF.Ln, bias=EPSB), reads=[("ps", bst)] + ["epsb"], writes=[("rstd", hf)])
                    sch.op("act", ACT(rstd[:, hsl], rstd[:, hsl], AF.Exp, scale=-0.5), reads=[("rstd", hf)], writes=[("rstd", hf)])
                    for kc in range(KC):
                        sch.op("dve",
                               STT(hT[:, kc, hsl], xt[:, kc, :], c1[:, C_NMW + kc:C_NMW + kc + 1], rstd[:, hsl],
                                   ALU.mult, ALU.mult),
                               reads=["xt", ("rstd", hf), "c1"], writes=[("hT", kc)])
                sch.mark("j%d_fm" % j)
                hTr = [("hT", kc) for kc in range(KC)]
                w1r = [("w1", i) for i in range(4)]
                for m in range(10):
                    b = rbank()
                    for kc in range(KC):
                        sch.op("pe", MM(ps[b][:], w1s[:, kc, m * 128:(m + 1) * 128], hT[:, kc, :], kc == 0,
                                        kc == KC - 1), reads=[("hT", kc), ("w1", kc // 4)], writes=[("ps", b)])
                    h = m % 2
                    if m < 2:
                        sch.op("act", ACT(QT[:, h, :], ps[b][:], AF.Copy), reads=[("ps", b)], writes=[("QT", h)])
                    elif m < 4:
                        sch.op("dve", CP(KT[:, h, j * 512:(j + 1) * 512], ps[b][:]), reads=[("ps", b)],
                               writes=[("KT", h, j)])
                    elif m < 6:
                        sch.op("act", ACT(hqT[:, h, :], ps[b][:], AF.Copy), reads=[("ps", b)],
                               writes=[("hqT", h)])
                    elif m < 8:
                        sch.op("act", ACT(snT[:, h, :], ps[b][:], AF.Sigmoid, scale=-1.0), reads=[("ps", b)],
                               writes=[("snT", h)])
                    else:
                        sch.op("act", ACT(sgT[:, h, :], ps[b][:], AF.Silu), reads=[("ps", b)],
                               writes=[("sgT", h)])
                sch.mark("j%d_tm" % j)
                for sub in range(4):
                    ba = rbank()
                    bb = rbank()
                    for kc in range(KC):
                        sch.op("pe", MM(ps[ba][:], hT[:, kc, sub * 128:(sub + 1) * 128], w1s[:, kc, 1280:1792],
                                        kc == 0, kc == KC - 1), reads=[("hT", kc), ("w1", kc // 4)],
                               writes=[("ps", ba)])
                    for kc in range(KC):
                        sch.op("pe", MM(ps[bb][:, 0:256], hT[:, kc, sub * 128:(sub + 1) * 128],
                                        w1s[:, kc, 768:1024], kc == 0, kc == KC - 1),
                               reads=[("hT", kc), ("w1", kc // 4)], writes=[("ps", bb)])
                    sch.op("act", ACT(Vt[:, j * 4 + sub, :], ps[ba][:, 0:256], AF.Copy), reads=[("ps", ba)],
                           writes=[("Vt", j * 4 + sub)])
                    sch.op("act", ACT(vhg[:, sub, :], ps[ba][:, 256:512], AF.Copy), reads=[("ps", ba)],
                           writes=[("vhg", sub)])
                    sch.op("act", ACT(ftm[:, sub, :], ps[bb][:, 0:256], AF.Sigmoid), reads=[("ps", bb)],
                           writes=[("ftm", sub)])
                    sch.op("dve", TT(ftm[:, sub, :], ftm[:, sub, :], lbt[:, 1, :], ALU.mult),
                           reads=[("ftm", sub), "lbt"], writes=[("ftm", sub)])
                    sch.op("dve", TT(ftm[:, sub, :], ftm[:, sub, :], lbt[:, 0, :], ALU.add),
                           reads=[("ftm", sub), "lbt"], writes=[("ftm", sub)])
                    sch.op("pool", TS(kk[:, sub, :], ftm[:, sub, :], -1.0, 1.0, ALU.mult, ALU.add),
                           reads=[("ftm", sub)], writes=[("kk", sub)])
                    sch.op("act", ACT(ftm[:, sub, :], ftm[:, sub, :], AF.Ln), reads=[("ftm", sub)],
                           writes=[("ftm", sub)])
                sch.mark("j%d_hgpro" % j)
                bG = [2, 3]
                for sub in range(4):
                    for h in range(2):
                        bd = rbank()
                        sch.op("pe", MM(ps[bd][:, 0:128], TD, ftm[:, sub, h * 128:(h + 1) * 128]),
                               reads=[("ftm", sub), "c1"], writes=[("ps", bd)])
                        sch.op("pe", MM(ps[bG[h]][:, sub * 128:(sub + 1) * 128], ftm[:, sub, h * 128:(h + 1) * 128],
                                        TG), reads=[("ftm", sub), "c1"], writes=[("ps", bG[h])])
                        sch.op("act", ACT(eD[:, h, :], ps[bd][:, 0:128], AF.Exp), reads=[("ps", bd)],
                               writes=[("eD", h)])
                        sch.op("dve", TT(kdec[:, sub, h * 128:(h + 1) * 128], kk[:, sub, h * 128:(h + 1) * 128],
                                         eD[:, h, :], ALU.mult), reads=[("kk", sub), ("eD", h)],
                               writes=[("kdec", sub, h)])
                for h in range(2):
                    sch.op("act", ACT(eG[:, h, :], ps[bG[h]][:], AF.Exp), reads=[("ps", bG[h])],
                           writes=[("eG", h)])
                    sch.op("act", ACT(Ssb[:, h, :], ps[bG[h]][:], AF.Exp, scale=-1.0), reads=[("ps", bG[h])],
                           writes=[("Ssb", h)])
                    sch.op("dve", TT(qh[:, h, :], hqT[:, h, :], eG[:, h, :], ALU.mult),
                           reads=[("hqT", h), ("eG", h)], writes=[("qh", h)])
                    sch.op("dve", STT(kth[:, h, :], snT[:, h, :], lbc[:, 2 + h:3 + h], Ssb[:, h, :], ALU.mult,
                                      ALU.mult), reads=[("snT", h), ("Ssb", h), "lbc"], writes=[("kth", h)])
                for h in range(2):
                    for sub in range(4):
                        ba = rbank()
                        sl = slice(sub * 128, (sub + 1) * 128)
                        sch.op("pe", MM(ps[ba][:, 0:128], kth[:, h, sl], qh[:, h, sl]),
                               reads=[("kth", h), ("qh", h)], writes=[("ps", ba)])
                        sch.op("dve", TT(Am[:, h, sl], ps[ba][:, 0:128], TG, ALU.mult), reads=[("ps", ba), "c1"],
                               writes=[("Am", h, sub)])
                sch.mark("j%d_hgchain" % j)
                for c in range(8):
                    sub, half = c // 2, c % 2
                    rows = slice(half * 64, half * 64 + 64)
                    cols = slice(c * 64, c * 64 + 64)
                    for h in range(2):
                        hs = slice(h * 128, (h + 1) * 128)
                        sch.op("pe", MM(ps[h][:, cols], Sbf[:, h, :], qh[:, h, cols], True, False),
                               reads=[("Sbf", h), ("qh", h)], writes=[("ps", h)])
                        sch.op("pe", MM(ps[h][:, cols], vhg[rows, sub, hs],
                                        Am[rows, h, sub * 128 + half * 64:sub * 128 + half * 64 + 64], False, True),
                               reads=[("vhg", sub), ("Am", h, sub)], writes=[("ps", h)])
                        bp = rbank()
                        sch.op("pe", MM(ps[bp][:, 0:128], kdec[rows, sub, hs], vhg[rows, sub, hs]),
                               reads=[("kdec", sub, h), ("vhg", sub)], writes=[("ps", bp)])
                        gl = eG[:, h, c * 64 + 63:c * 64 + 64]
                        sch.op("dve", STT(Sbf[:, h, :], Sf[:, h, :], gl, ps[bp][:, 0:128], ALU.mult, ALU.add),
                               reads=[("Sf", h), ("eG", h), ("ps", bp)], writes=[("Sbf", h)])
                        sch.op("dve", STT(Sf[:, h, :], Sf[:, h, :], gl, ps[bp][:, 0:128], ALU.mult, ALU.add),
                               reads=[("Sf", h), ("eG", h), ("ps", bp)], writes=[("Sf", h)])
                sch.mark("j%d_hgout" % j)
                mo = mixo
                for h in range(2):
                    sch.op("act", ACT(sq[:, h, :], ps[h][:], AF.Square), reads=[("ps", h)], writes=[("sq", h)])
                    bs = rbank()
                    sch.op("pe", MM(ps[bs][:], onesH[:], sq[:, h, :]), reads=[("sq", h), "onesH"],
                           writes=[("ps", bs)])
                    sch.op("act", ACT(r12[:, h, :], ps[bs][:], AF.Ln, bias=EPSB), reads=[("ps", bs)] + ["epsb"], writes=[("r12", h)])
                    sch.op("act", ACT(r12[:, h, :], r12[:, h, :], AF.Exp, scale=-0.5), reads=[("r12", h)], writes=[("r12", h)])
                    sch.op("dve", STT(Ssb[:, h, :], ps[h][:], c1[:, C_HGW:C_HGW + 1], r12[:, h, :], ALU.mult,
                                      ALU.mult), reads=[("ps", h), ("r12", h), "c1"], writes=[("Ssb", h)])
                    sch.op("pool", TT(mo[:, 2 + h, :], Ssb[:, h, :], sgT[:, h, :], ALU.mult),
                           reads=[("Ssb", h), ("sgT", h)], writes=[("mixo", 2 + h)])
                sch.mark("j%d_att" % j)
                for h in range(2):
                    hs = slice(h * 128, (h + 1) * 128)
                    nk = 4 * j + 4
                    for kt in range(nk):
                        delta = 512 * j - 128 * kt
                        far = delta >= 256
                        eb = (kt % 2) * 2
                        bS = [rbank(), rbank()]
                        for m in range(2):
                            dsl = slice(m * 64, m * 64 + 64)
                            sch.op("pe", MM(ps[bS[m]][:], KT[dsl, h, kt * 128:(kt + 1) * 128], QT[dsl, h, :]),
                                   reads=[("KT", h, kt // 4), ("QT", h)], writes=[("ps", bS[m])])
                        for m in range(2):
                            if far:
                                sch.op("act", ACT(Eb[:, eb + m, :], ps[bS[m]][:], AF.Exp,
                                                  bias=c1[:, C_BFAR + h:C_BFAR + h + 1], scale=SCALE),
                                       reads=[("ps", bS[m]), "c1"], writes=[("Eb", eb + m)])
                            else:
                                c0 = C_BIAS + 1024 * h + delta + 384
                                sch.op("dve", STT(Ssb[:, m, :], ps[bS[m]][:], SCALE, c1[:, c0:c0 + 512], ALU.mult,
                                                  ALU.add), reads=[("ps", bS[m]), "c1"], writes=[("Ssb", m)])
                                sch.op("act", ACT(Eb[:, eb + m, :], Ssb[:, m, :], AF.Exp), reads=[("Ssb", m)],
                                       writes=[("Eb", eb + m)])
                        for m in range(2):
                            sch.op("pe", MM(ps[m][:], Vt[:, kt, hs], Eb[:, eb + m, :], kt == 0, kt == nk - 1),
                                   reads=[("Vt", kt), ("Eb", eb + m)], writes=[("ps", m)])
                            sch.op("pe", MM(ps[2 + m][:], ones[:], Eb[:, eb + m, :], kt == 0, kt == nk - 1),
                                   reads=["ones", ("Eb", eb + m)], writes=[("ps", 2 + m)])
                    for m in range(2):
                        sch.op("dve", lambda e, m=m: e.reciprocal(r12[:, m, :], ps[2 + m][:]),
                               reads=[("ps", 2 + m)], writes=[("r12", m)])
                    sch.op("dve", TT(Ssb[:, 0, :], ps[0][:], r12[:, 0, :], ALU.mult),
                           reads=[("ps", 0), ("r12", 0)], writes=[("Ssb", 0)])
                    sch.op("dve", STT(Ssb[:, 1, :], ps[1][:], NLAM, r12[:, 1, :], ALU.mult, ALU.mult),
                           reads=[("ps", 1), ("r12", 1), "sm"], writes=[("Ssb", 1)])
                    sch.op("dve", TT(Ssb[:, 0, :], Ssb[:, 0, :], Ssb[:, 1, :], ALU.add),
                           reads=[("Ssb", 0), ("Ssb", 1)], writes=[("Ssb", 0)])
                    sch.op("act", ACT(sq[:, 0, :], Ssb[:, 0, :], AF.Square), reads=[("Ssb", 0)], writes=[("sq", 0)])
                    bs = rbank()
                    sch.op("pe", MM(ps[bs][:], onesH[:], sq[:, 0, :]), reads=[("sq", 0), "onesH"],
                           writes=[("ps", bs)])
                    sch.op("act", ACT(r12[:, 0, :], ps[bs][:], AF.Ln, bias=EPSB), reads=[("ps", bs)] + ["epsb"], writes=[("r12", 0)])
                    sch.op("act", ACT(r12[:, 0, :], r12[:, 0, :], AF.Exp, scale=-0.5), reads=[("r12", 0)], writes=[("r12", 0)])
                    sch.op("dve", STT(mo[:, h, :], Ssb[:, 0, :], SUBW8, r12[:, 0, :], ALU.mult, ALU.mult),
                           reads=[("Ssb", 0), ("r12", 0), "sm"], writes=[("mixo", h)])
                sch.mark("j%d_store" % j)
                for c in range(4):
                    sch.dma("sp", "ibw%d" % c, DMA(ib[c * 128:(c + 1) * 128, 2 + j * 512:2 + (j + 1) * 512], mo[:, c, :]),
                            reads=[("mixo", c)], writes=[("ib", j, c)])

            ibres = [("ib", j, c) for j in range(NJ) for c in range(4)] + [("ibz", c) for c in range(4)]
            if dbg in ("mix", "sim"):
                sch.dma("sp", "dbg", DMA(dmix, ib), reads=ibres, writes=["dmix"], force=True)
                MARKS[:] = sch.marks
                sch.final_wait("sp", ["dbg"])
            if dbg != "sim":
                sch.dma("pool", "cc",
                        lambda e: e.collective_compute("AllGather", ALU.bypass, replica_groups=[list(range(8))],
                                                       ins=[ib], outs=[ob]),
                        reads=ibres, writes=["ob"], inc=1)
            sch.barrier(skip=("cc",))
            with nc.Block() as block:
                sch.flush(block)

        if dbg == "sim":
            return nc
        if dbg == "mix":
            return nc

        with ExitStack() as st:
            def sb(name, shape, dt=F32):
                return st.enter_context(nc.sbuf_tensor(name, list(shape), dt))

            xs = sb("xs", [128, KC, NT2])
            actb = sb("actb", [128, KC, NT2], BF16)
            mqg = sb("mqg", [128, KC, NT2], BF16)
            wr = sb("wr", [128, NSLOT, KC, 256], BF16)
            mkT = sb("mkT", [128, KC, NMEM], BF16)
            mv = sb("mv", [128, 2, D], BF16)
            Em = sb("Em", [128, 2, 2, 512], BF16)
            rz = sb("rz", [128, 512])
            rs2 = sb("rs2", [128, NT2])
            sq2 = sb("sq2", [128, 2, 512], BF16)
            scr = sb("scr", [128, 4, NT2])
            c2 = sb("c2", [128, C2])
            ones2 = sb("ones2", [128, 128], BF16)
            onesD2 = sb("onesD2", [128, 128], BF16)

            epsb2 = sb("epsb2", [128, 1])
            EPSB = epsb2[:, 0:1]
            sch.op("dve", lambda e: e.memset(epsb2[:], EPS), writes=["epsb"])
            sch.op("dve", lambda e: e.memset(ones2[:], 1.0), writes=["ones2"])
            sch.op("dve", lambda e: e.memset(onesD2[:], 1.0 / D), writes=["onesD2"])
            sch.dma("sp", "c2", DMA(c2[:], cst2), writes=["c2"])
            sch.dma("sp", "x2", DMA(xs[:], x2T), writes=[("x", m) for m in range(KC)])
            for a in range(4):
                sch.dma("sp", "memT", DMA(scr[:, a, 0:1024], memT[:, 4 * a:4 * a + 4, :].rearrange("p k n -> p (k n)")),
                        writes=["scr"])

            def memv(kc, c0=0, n=NMEM):
                return scr[:, kc // 4, (kc % 4) * 256 + c0:(kc % 4) * 256 + c0 + n]

            wslot = [0]

            def wload(src_ap, nk=KC):
                s_ = wslot[0] % NSLOT
                wslot[0] += 1
                sch.dma("pool", "wr%d" % s_, DMA(wr[:, s_, 0:nk, :], src_ap), writes=[("wr", s_)])
                return s_

            def rms_stats(src_fn, tiles, res_reads, onesm, nkc=KC):
                for (c0, n) in tiles:
                    b = rbank(0, 8)
                    for kc in range(nkc):
                        sch.op("act", ACT(sq2[:, kc % 2, 0:n], src_fn(kc, c0, n), AF.Square), reads=res_reads(kc),
                               writes=[("sq2", kc % 2)])
                        sch.op("pe", MM(ps[b][:, 0:n], onesm[:], sq2[:, kc % 2, 0:n], kc == 0, kc == nkc - 1),
                               reads=[("sq2", kc % 2), "ones2", "onesD2"], writes=[("ps", b)])
                    sch.op("act", ACT(rs2[:, c0:c0 + n], ps[b][:, 0:n], AF.Ln, bias=EPSB), reads=[("ps", b)] + ["epsb"], writes=[("rs2", c0)])
                    sch.op("act", ACT(rs2[:, c0:c0 + n], rs2[:, c0:c0 + n], AF.Exp, scale=-0.5), reads=[("rs2", c0)], writes=[("rs2", c0)])

            rms_stats(lambda kc, c0, n: memv(kc, c0, n), [(0, NMEM)], lambda kc: ["scr"], onesD2)
            for kc in range(KC):
                sch.op("dve", STT(actb[:, kc, 0:NMEM], memv(kc), c2[:, C2_NKV + kc:C2_NKV + kc + 1],
                                  rs2[:, 0:NMEM], ALU.mult, ALU.mult), reads=["scr", "c2", ("rs2", 0)],
                       writes=[("act", kc)])
            for blk in range(8):
                s_ = wload(wkv_d[blk])
                for mi in range(2):
                    m = blk * 2 + mi
                    b = rbank(0, 8)
                    for kc in range(KC):
                        sch.op("pe", MM(ps[b][:, 0:NMEM], wr[:, s_, kc, mi * 128:(mi + 1) * 128], actb[:, kc, 0:NMEM],
                                        kc == 0, kc == KC - 1), reads=[("wr", s_), ("act", kc)], writes=[("ps", b)])
                    sch.op("act", ACT(mkT[:, m, :], ps[b][:, 0:NMEM], AF.Copy), reads=[("ps", b)],
                           writes=[("mkT", m)])
            for blk in range(8):
                s_ = wload(wkv_d[8 + blk])
                for sub in range(2):
                    b = rbank(0, 8)
                    for kc in range(KC):
                        sch.op("pe", MM(ps[b][:, 0:256], actb[:, kc, sub * 128:(sub + 1) * 128], wr[:, s_, kc, :],
                                        kc == 0, kc == KC - 1), reads=[("wr", s_), ("act", kc)], writes=[("ps", b)])
                    sch.op("dve", CP(mv[:, sub, blk * 256:(blk + 1) * 256], ps[b][:, 0:256]), reads=[("ps", b)],
                           writes=[("mv", sub, blk)])

            pidc = {}

            def mixdma(e, kc):
                if "rb" not in pidc:
                    pid = nc.partition_id([e.engine])
                    pidc["rb"] = (pid // 4) * 2048
                    pidc["cb"] = (pid % 4) * 1024
                src = ob[bass.ds(pidc["rb"] + kc * 512, 512), bass.ds(pidc["cb"], NT2)].rearrange(
                    "(k p) n -> p k n", p=128)
                return e.dma_start(out=actb[:, 4 * kc:4 * kc + 4, :], in_=src)

            for kc in range(4):
                sch.dma("pool", "mixl", lambda e, kc=kc: mixdma(e, kc), reads=["ob"],
                        writes=[("act", k2) for k2 in range(KC)])

            def proj(wd, nblk, src, src_key, evac):
                for blk in range(nblk):
                    s_ = wload(wd[blk])
                    for mi in range(2):
                        m = blk * 2 + mi
                        bks = [rbank(0, 8) for _ in TILES2]
                        for kc in range(KC):
                            for ti, (c0, n) in enumerate(TILES2):
                                sch.op("pe", MM(ps[bks[ti]][:, 0:n], wr[:, s_, kc, mi * 128:(mi + 1) * 128],
                                                src[:, kc, c0:c0 + n], kc == 0, kc == KC - 1),
                                       reads=[("wr", s_), (src_key, kc)], writes=[("ps", bks[ti])])
                        for ti, (c0, n) in enumerate(TILES2):
                            evac(m, ti, c0, n, bks[ti])

            def evac_resid(m, ti, c0, n, b):
                sch.op("dve", TT(xs[:, m, c0:c0 + n], xs[:, m, c0:c0 + n], ps[b][:, 0:n], ALU.add),
                       reads=[("ps", b), ("x", m)], writes=[("x", m)])

            def norm_to_act(wcol):
                rms_stats(lambda kc, c0, n: xs[:, kc, c0:c0 + n], TILES2, lambda kc: [("x", kc)], onesD2)
                for kc in range(KC):
                    sch.op("dve",
                           STT(actb[:, kc, :], xs[:, kc, :], c2[:, wcol + kc:wcol + kc + 1], rs2[:], ALU.mult,
                               ALU.mult), reads=[("x", kc), "c2"] + [("rs2", c0) for c0, _ in TILES2],
                           writes=[("act", kc)])

            proj(wo_d, 8, actb, "act", evac_resid)
            norm_to_act(C2_NMEM)

            def evac_mq(m, ti, c0, n, b):
                sch.op("act", ACT(mqg[:, m, c0:c0 + n], ps[b][:, 0:n], AF.Copy), reads=[("ps", b)],
                       writes=[("mqg", m)])

            proj(wq_d, 8, actb, "act", evac_mq)
            MS = 512.0 ** -0.5
            it = 0
            for h in range(4):
                for ti, (c0, n) in enumerate(TILES2):
                    eb = it % 2
                    it += 1
                    for jm in range(2):
                        b = rbank(0, 8)
                        for dc in range(4):
                            sch.op("pe", MM(ps[b][:, 0:n], mkT[:, 4 * h + dc, jm * 128:(jm + 1) * 128],
                                            mqg[:, 4 * h + dc, c0:c0 + n], dc == 0, dc == 3),
                                   reads=[("mkT", 4 * h + dc), ("mqg", 4 * h + dc)], writes=[("ps", b)])
                        sch.op("act", ACT(Em[:, eb, jm, 0:n], ps[b][:, 0:n], AF.Exp, scale=MS), reads=[("ps", b)],
                               writes=[("Em", eb, jm)])
                    bz = rbank(0, 8)
                    for jm in range(2):
                        sch.op("pe", MM(ps[bz][:, 0:n], ones2[:], Em[:, eb, jm, 0:n], jm == 0, jm == 1),
                               reads=[("Em", eb, jm), "ones2"], writes=[("ps", bz)])
                    sch.op("dve", lambda e, n=n, bz=bz: e.reciprocal(rz[:, 0:n], ps[bz][:, 0:n]),
                           reads=[("ps", bz)], writes=["rz"])
                    for dc in range(4):
                        b = rbank(0, 8)
                        mcol = (4 * h + dc) * 128
                        for jm in range(2):
                            sch.op("pe", MM(ps[b][:, 0:n], mv[:, jm, mcol:mcol + 128], Em[:, eb, jm, 0:n], jm == 0,
                                            jm == 1), reads=[("mv", jm, (4 * h + dc) // 2), ("Em", eb, jm)],
                                   writes=[("ps", b)])
                        sch.op("dve", TT(actb[:, 4 * h + dc, c0:c0 + n], ps[b][:, 0:n], rz[:, 0:n], ALU.mult),
                               reads=[("ps", b), "rz"], writes=[("act", 4 * h + dc)])
            proj(wmo_d, 8, actb, "act", evac_resid)
            norm_to_act(C2_NFFN)
            FLAG = c2[:, C2_FLAG:C2_FLAG + 1]
            ch0 = 0
            for gi, gc in enumerate(GROUPS):
                for lb_ in range(gc // 2):
                    blk = ch0 // 2 + lb_
                    sa = wload(wup_d[blk])
                    sb_ = wload(wup_d[22 + blk])
                    for mi in range(2):
                        lc = lb_ * 2 + mi
                        ch = ch0 + lc
                        for ab, s_ in ((0, sa), (1, sb_)):
                            chn = ch + 44 * ab
                            U = scr[:, ab, :]
                            Y = scr[:, 2 + ab, :]
                            bks = [rbank(0, 8) for _ in TILES2]
                            for kc in range(KC):
                                for ti, (c0, n) in enumerate(TILES2):
                                    sch.op("pe", MM(ps[bks[ti]][:, 0:n], wr[:, s_, kc, mi * 128:(mi + 1) * 128],
                                                    actb[:, kc, c0:c0 + n], kc == 0, kc == KC - 1),
                                           reads=[("wr", s_), ("act", kc)], writes=[("ps", bks[ti])])
                            for ti, (c0, n) in enumerate(TILES2):
                                if ti == 2:
                                    sch.op("act", ACT(U[:, c0:c0 + n], ps[bks[ti]][:, 0:n], AF.Copy, scale=FLAG),
                                           reads=[("ps", bks[ti]), "c2"], writes=[("U", ab, ti)])
                                else:
                                    sch.op("act", ACT(U[:, c0:c0 + n], ps[bks[ti]][:, 0:n], AF.Copy),
                                           reads=[("ps", bks[ti])], writes=[("U", ab, ti)])
                            ur = [("U", ab, ti) for ti in range(3)]
                            cw = lambda k, chn=chn: c2[:, C2_CW + chn * 3 + k:C2_CW + chn * 3 + k + 1]
                            sch.op("act", ACT(Y[:, 0:1024], U[:, 2:1026], AF.Identity,
                                              bias=c2[:, C2_CB + chn:C2_CB + chn + 1], scale=cw(2)),
                                   reads=ur + ["c2"], writes=[("Y", ab)])
                            sch.op("dve", STT(Y[:, 0:1024], U[:, 1:1025], cw(1), Y[:, 0:1024], ALU.mult, ALU.add),
                                   reads=ur + ["c2", ("Y", ab)], writes=[("Y", ab)])
                            sch.op("dve", STT(Y[:, 0:1024], U[:, 0:1024], cw(0), Y[:, 0:1024], ALU.mult, ALU.add),
                                   reads=ur + ["c2", ("Y", ab)], writes=[("Y", ab)])
                        sch.op("act", ACT(scr[:, 2, 0:1024], scr[:, 2, 0:1024], AF.Silu), reads=[("Y", 0)],
                               writes=[("Y", 0)])
                        sch.op("dve", TT(mqg[:, lc, 0:1024], scr[:, 2, 0:1024], scr[:, 3, 0:1024], ALU.mult),
                               reads=[("Y", 0), ("Y", 1)], writes=[("mqg", lc)])
                for cbk in range(8):
                    s_ = wload(wdn_d[gi, cbk][:, 0:gc, :], nk=gc)
                    for mi in range(2):
                        m = cbk * 2 + mi
                        bks = [rbank(0, 8) for _ in range(2)]
                        for lc in range(gc):
                            for ti in range(2):
                                sch.op("pe", MM(ps[bks[ti]][:], wr[:, s_, lc, mi * 128:(mi + 1) * 128],
                                                mqg[:, lc, ti * 512:(ti + 1) * 512], lc == 0, lc == gc - 1),
                                       reads=[("wr", s_), ("mqg", lc)], writes=[("ps", bks[ti])])
                        for ti in range(2):
                            c0 = 2 + ti * 512
                            sch.op("dve", TT(xs[:, m, c0:c0 + 512], xs[:, m, c0:c0 + 512], ps[bks[ti]][:], ALU.add),
                                   reads=[("ps", bks[ti]), ("x", m)], writes=[("x", m)])
                ch0 += gc
            rms_stats(lambda kc, c0, n: xs[:, kc, c0:c0 + n], TILES2[0:2], lambda kc: [("x", kc)], onesD2)
            for m in range(KC):
                yo = scr[:, m % 2, 0:1024]
                sch.op("dve",
                       STT(yo, xs[:, m, 2:1026], c2[:, C2_NFIN + m:C2_NFIN + m + 1], rs2[:, 2:1026], ALU.mult,
                           ALU.mult), reads=[("x", m), "c2", ("rs2", 2), ("rs2", 514)], writes=[("U", m % 2, 0)])
                sch.dma("sp", "out%d" % (m % 2), DMA(yT[:, m, :], yo), reads=[("U", m % 2, 0)], writes=[("yT", m)])
            sch.final_wait("sp", ["out0", "out1"])
            with nc.Block() as block:
                sch.flush(block)
    return nc


def _bucket(n):
    n = np.maximum(n, 0)
    nf = np.maximum(n, 1).astype(np.float32)
    large = 16 + (np.log(nf / np.float32(16)) / np.float32(math.log(128 / 16)) * np.float32(16)).astype(np.int32)
    large = np.minimum(large, 31)
    return np.where(n < 16, n, large)


def _pk(a):
    k = a.shape[0] // 128
    return np.ascontiguousarray(a.reshape(k, 128, -1).transpose(1, 0, 2))


def _blocks(w, bc=256):
    K, N = w.shape
    return np.ascontiguousarray(w.reshape(K // 128, 128, N // bc, bc).transpose(2, 1, 0, 3))


def _rep(v):
    return np.broadcast_to(np.asarray(v, np.float32).reshape(1, -1), (128, np.asarray(v).size))


def prep(inp):
    f32 = np.float32
    x = np.asarray(inp["x"], f32)
    mem = np.asarray(inp["mem"], f32)
    w_in = np.asarray(inp["w_in"], f32)[0]
    w_out = np.asarray(inp["w_out"], f32)[0]
    rel_bias = np.asarray(inp["rel_bias"], f32)
    lbraw = np.asarray(inp["hg_lb_raw"], f32)
    cc = np.arange(1024)[None, :]
    kp = np.arange(128)[:, None]
    n = cc - 384 - kp
    bidx = _bucket(n)
    maskw = np.where(n < 0, f32(NEG), f32(0.0)).astype(f32)
    ti = np.arange(128)
    same = (ti[:, None] // 64) == (ti[None, :] // 64)
    TG = (same & (ti[:, None] <= ti[None, :])).astype(f32)
    TD = (same & (ti[:, None] > ti[None, :])).astype(f32)
    perm = []
    for g in range(4):
        for h in (2 * g, 2 * g + 1):
            perm += list(range(h * 128, (h + 1) * 128))
        for h in (2 * g, 2 * g + 1):
            perm += list(range(1024 + h * 128, 1024 + (h + 1) * 128))
    perm = np.array(perm)
    wo_b = _blocks(w_out[perm, :])
    wq_b = _blocks(np.asarray(inp["w_mq"], f32)[0])
    wkv_b = _blocks(np.asarray(inp["w_mkv"], f32)[0])
    wmo_b = _blocks(np.asarray(inp["w_mo"], f32)[0])
    wup_b = _blocks(np.asarray(inp["w_up"], f32)[0])
    w_down = np.asarray(inp["w_down"], f32)[0]
    wdn_b = np.zeros((4, 8, 128, 12, 256), f32)
    r0 = 0
    for gi, gc in enumerate(GROUPS):
        wdn_b[gi, :, :, 0:gc, :] = _blocks(w_down[r0 * 128:(r0 + gc) * 128, :])
        r0 += gc
    conv_w = np.asarray(inp["conv_w"], f32)[0]
    conv_b = np.asarray(inp["conv_b"], f32)[0]
    cwl = np.ascontiguousarray(conv_w.reshape(3, 88, 128).transpose(2, 1, 0)).reshape(128, 264)
    cbl = np.ascontiguousarray(conv_b.reshape(88, 128).T)

    def pk1(v):
        return np.ascontiguousarray(np.asarray(v, f32).reshape(16, 128).T)

    xTb = [np.ascontiguousarray(x[b].T.reshape(KC, 128, 8, 512).transpose(2, 1, 0, 3)) for b in range(2)]
    memTb = [_pk(np.ascontiguousarray(mem[b].T)) for b in range(2)]
    maps = []
    for c in range(8):
        b, g = c // 4, c % 4
        hs = (2 * g, 2 * g + 1)
        cols = []
        for base in (0, 1024, 3072, 4096, 6144):
            for h in hs:
                cols += list(range(base + h * 128, base + (h + 1) * 128))
        for base in (2048, 5120):
            for h in hs:
                cols += list(range(base + h * 128, base + (h + 1) * 128))
        w1 = _pk(w_in[:, np.array(cols)])
        c1 = np.zeros((128, C1), f32)
        c1[:, C_NMW:C_NMW + 16] = pk1(inp["norm_mix_w"][0])
        for i, nm in enumerate(("lam_q1", "lam_k1", "lam_q2", "lam_k2")):
            c1[:, C_LAM + 64 * i:C_LAM + 64 * (i + 1)] = _rep(inp[nm][0])
        c1[:, C_SUBW] = np.asarray(inp["da_subln_w"], f32)[0]
        c1[:, C_HGW] = np.asarray(inp["hg_norm_w"], f32)[0]
        for l in range(2):
            for hi, h in enumerate(hs):
                c1[:, C_LBCM + 2 * l + hi] = lbraw[l, h * 128:(h + 1) * 128]
                c1[:, C_LBTM + 256 * l + 128 * hi:C_LBTM + 256 * l + 128 * (hi + 1)] = _rep(
                    lbraw[l, h * 128:(h + 1) * 128])
        for hi, h in enumerate(hs):
            c1[:, C_BFAR + hi] = rel_bias[31, h]
            c1[:, C_BIAS + 1024 * hi:C_BIAS + 1024 * (hi + 1)] = rel_bias[bidx, h]
        c1[:, C_TG:C_TG + 128] = TG
        c1[:, C_TD:C_TD + 128] = TD
        q = g
        t0 = 1024 * q
        xsl = np.zeros((NT2, D), f32)
        if q == 0:
            xsl[2:] = x[b, 0:1024]
        else:
            xsl = x[b, t0 - 2:t0 + 1024]
        x2 = _pk(np.ascontiguousarray(xsl.T))
        c2 = np.zeros((128, C2), f32)
        c2[:, C2_NMEM:C2_NMEM + 16] = pk1(inp["norm_mem_w"][0])
        c2[:, C2_NKV:C2_NKV + 16] = pk1(inp["mem_kv_norm_w"][0])
        c2[:, C2_NFFN:C2_NFFN + 16] = pk1(inp["norm_ffn_w"][0])
        c2[:, C2_NFIN:C2_NFIN + 16] = pk1(inp["final_norm_w"])
        c2[:, C2_FLAG] = 0.0 if q == 0 else 1.0
        c2[:, C2_CB:C2_CB + 88] = cbl
        c2[:, C2_CW:C2_CW + 264] = cwl
        maps.append({"xT1": xTb[b], "w1": w1, "cst1": c1, "maskw": maskw, "x2T": x2, "memT": memTb[b], "cst2": c2,
                     "w_out": wo_b, "w_mq": wq_b, "w_mkv": wkv_b, "w_mo": wmo_b, "w_up": wup_b,
                     "w_down": wdn_b})
    return maps


_NC = None


def kernel(**inputs):
    global _NC
    maps = prep(inputs)
    if _NC is None:
        _NC = build()
    res = run_bass_kernel_spmd(_NC, maps, core_ids=list(range(8)))
    out = np.zeros((2, S, D), np.float32)
    for c in range(8):
        b, q = c // 4, c % 4
        y = np.asarray(res.results[c]["yT"])
        out[b, 1024 * q:1024 * (q + 1), :] = y.transpose(2, 1, 0).reshape(1024, D)
    return out
```
